# Optimizing a Trainium2 kernel written in Bass

```python
import math
import jax, jax.numpy as jnp
from jax import lax
import numpy as np

D_MODEL = 2048
BATCH = 4
SEQ = 2048
DEPTH = 2

ROPE_THETA = 10000.0
BLOCK_Q = 128
D_FF = 5632
DN_ALPHA = (2 * DEPTH) ** 0.25
DN_BETA = (8 * DEPTH) ** (-0.25)
LN_EPS = 1e-5
RMS_EPS = 1e-6
NEG = -1e30
FORCE_SCORE = 1e9
N_EVEN = (DEPTH + 1) // 2
N_ODD = DEPTH // 2

MLA_HEADS = 8
MLA_NOPE = 128
MLA_ROPE = 64
MLA_V = 128
MLA_Q_RANK = 512
MLA_KV_RANK = 512

SWA_HEADS = 16
SWA_KV_HEADS = 2
SWA_HEAD_DIM = 64
SWA_WINDOW = 128

EVEN_IN = MLA_Q_RANK + MLA_KV_RANK + MLA_ROPE + (SWA_HEADS + 2 * SWA_KV_HEADS) * SWA_HEAD_DIM
EVEN_OUT = MLA_HEADS * MLA_V + SWA_HEADS * SWA_HEAD_DIM

NSA_HEADS = 16
NSA_KV_GROUPS = 2
NSA_HEAD_DIM = 128
CMP_BLOCK = 32
CMP_STRIDE = 16
CMP_HIDDEN = 256
SLC_BLOCK = 64
N_SEL = 8
NSA_WINDOW = 512

ODD_IN = NSA_HEADS * NSA_HEAD_DIM + 6 * NSA_KV_GROUPS * NSA_HEAD_DIM + 3 * NSA_HEADS
ODD_OUT = NSA_HEADS * NSA_HEAD_DIM

kernel_name = "hybrid_mla_swa_nsa_macaron_deepnorm"

F32 = jnp.float32


def _offsets(sizes):
    return [int(v) for v in np.cumsum(sizes)]


def layer_norm(x, g, b):
    xf = x.astype(F32)
    mu = xf.mean(-1, keepdims=True)
    var = jnp.square(xf - mu).mean(-1, keepdims=True)
    return ((xf - mu) * lax.rsqrt(var + LN_EPS) * g.astype(F32) + b.astype(F32)).astype(x.dtype)


def rms_norm(x, g):
    xf = x.astype(F32)
    return (xf * lax.rsqrt(jnp.mean(xf * xf, -1, keepdims=True) + RMS_EPS) * g.astype(F32)).astype(x.dtype)


def swiglu(x, wg, wu, wd):
    return (jax.nn.silu(x @ wg) * (x @ wu)) @ wd


def rope_tables(seq, dim):
    inv = 1.0 / (ROPE_THETA ** (jnp.arange(0, dim, 2, dtype=F32) / dim))
    ang = jnp.arange(seq, dtype=F32)[:, None] * inv[None, :]
    return jnp.cos(ang), jnp.sin(ang)


def apply_rope(x, cos, sin):
    x1, x2 = jnp.split(x, 2, axis=-1)
    c = cos[None, :, None, :].astype(x.dtype)
    s = sin[None, :, None, :].astype(x.dtype)
    return jnp.concatenate([x1 * c - x2 * s, x1 * s + x2 * c], axis=-1)


def causal_block_attention(q, k, v, scale):
    B, S, H, dq = q.shape
    nb = S // BLOCK_Q
    qb = q.reshape(B, nb, BLOCK_Q, H, dq).transpose(1, 0, 2, 3, 4)
    kpos = jnp.arange(S)

    def one(args):
        i, qi = args
        s = jnp.einsum('bqhd,bkhd->bhqk', qi, k).astype(F32) * scale
        qpos = i * BLOCK_Q + jnp.arange(BLOCK_Q)
        mask = kpos[None, :] <= qpos[:, None]
        p = jax.nn.softmax(jnp.where(mask, s, NEG), axis=-1)
        return jnp.einsum('bhqk,bkhd->bqhd', p.astype(v.dtype), v)

    out = lax.map(one, (jnp.arange(nb), qb))
    return out.transpose(1, 0, 2, 3, 4).reshape(B, S, H, v.shape[-1])


def banded_attention(q, k, v, window, scale, sinks=None):
    B, S, G, R, d = q.shape
    nb = S // BLOCK_Q
    pad = -(-(window - 1) // BLOCK_Q) * BLOCK_Q
    span = pad + BLOCK_Q
    kp = jnp.pad(k, ((0, 0), (pad, 0), (0, 0), (0, 0)))
    vp = jnp.pad(v, ((0, 0), (pad, 0), (0, 0), (0, 0)))
    idx = (np.arange(nb) * BLOCK_Q)[:, None] + np.arange(span)[None, :]
    kb = kp[:, idx]
    vb = vp[:, idx]
    qb = q.reshape(B, nb, BLOCK_Q, G, R, d)
    s = jnp.einsum('bnqgrd,bnkgd->bngrqk', qb, kb).astype(F32) * scale
    qpos = (np.arange(nb) * BLOCK_Q)[:, None] + np.arange(BLOCK_Q)[None, :]
    kpos = idx - pad
    dist = qpos[:, :, None] - kpos[:, None, :]
    mask = jnp.asarray((dist >= 0) & (dist < window) & (kpos[:, None, :] >= 0))
    s = jnp.where(mask[None, :, None, None], s, NEG)
    if sinks is not None:
        sl = jnp.broadcast_to(sinks.reshape(G, R)[None, None, :, :, None, None].astype(F32), s.shape[:-1] + (1,))
        p = jax.nn.softmax(jnp.concatenate([s, sl], axis=-1), axis=-1)[..., :-1]
    else:
        p = jax.nn.softmax(s, axis=-1)
    o = jnp.einsum('bngrqk,bnkgd->bnqgrd', p.astype(v.dtype), vb)
    return o.reshape(B, S, G, R, d)


def compress_tokens(x, pos_emb, w1, w2):
    B, S, G, d = x.shape
    nc = (S - CMP_BLOCK) // CMP_STRIDE + 1
    idx = (np.arange(nc) * CMP_STRIDE)[:, None] + np.arange(CMP_BLOCK)[None, :]
    xb = x[:, idx] + pos_emb[:, None, :].astype(x.dtype)
    xb = xb.transpose(0, 1, 3, 2, 4).reshape(B, nc, G, CMP_BLOCK * d)
    return jax.nn.gelu(xb @ w1) @ w2


def selected_block_attention(q, k, v, blk_idx, blk_valid, scale):
    B, S, G, R, d = q.shape
    nb = S // BLOCK_Q
    n_sel = blk_idx.shape[-1]
    kT = k.transpose(0, 2, 1, 3)
    vT = v.transpose(0, 2, 1, 3)
    gather = jax.vmap(jax.vmap(lambda a, i: a[i]))
    qb = q.reshape(B, nb, BLOCK_Q, G, R, d).transpose(1, 0, 2, 3, 4, 5)

    def to_blocks(a):
        return a.reshape(B, G, nb, BLOCK_Q, n_sel).transpose(2, 0, 1, 3, 4)

    offs = jnp.arange(SLC_BLOCK)

    def one(args):
        i, qi, bi, vi = args
        tok = (bi[..., None] * SLC_BLOCK + offs).reshape(B, G, BLOCK_Q, n_sel * SLC_BLOCK)
        qpos = i * BLOCK_Q + jnp.arange(BLOCK_Q)
        m = jnp.repeat(vi, SLC_BLOCK, axis=-1) & (tok <= qpos[:, None])
        kg = gather(kT, tok)
        vg = gather(vT, tok)
        s = jnp.einsum('bqgrd,bgqnd->bgrqn', qi, kg).astype(F32) * scale
        m = m[:, :, None]
        p = jax.nn.softmax(jnp.where(m, s, NEG), axis=-1) * m
        return jnp.einsum('bgrqn,bgqnd->bqgrd', p.astype(v.dtype), vg)

    out = lax.map(one, (jnp.arange(nb), qb, to_blocks(blk_idx), to_blocks(blk_valid)))
    return out.transpose(1, 0, 2, 3, 4, 5).reshape(B, S, G, R, d)


def even_mixer(h, w_in, q_norm, w_uq, kv_norm, w_ukv, sinks, w_out):
    B, S, _ = h.shape
    z = h @ w_in
    c_q, c_kv, k_pe, q_s, k_s, v_s = jnp.split(z, _offsets([MLA_Q_RANK, MLA_KV_RANK, MLA_ROPE, SWA_HEADS * SWA_HEAD_DIM, SWA_KV_HEADS * SWA_HEAD_DIM]), axis=-1)
    cos_r, sin_r = rope_tables(S, MLA_ROPE)
    q = (rms_norm(c_q, q_norm) @ w_uq).reshape(B, S, MLA_HEADS, MLA_NOPE + MLA_ROPE)
    q = jnp.concatenate([q[..., :MLA_NOPE], apply_rope(q[..., MLA_NOPE:], cos_r, sin_r)], axis=-1)
    kv = (rms_norm(c_kv, kv_norm) @ w_ukv).reshape(B, S, MLA_HEADS, MLA_NOPE + MLA_V)
    k_pe = apply_rope(k_pe.reshape(B, S, 1, MLA_ROPE), cos_r, sin_r)
    k = jnp.concatenate([kv[..., :MLA_NOPE], jnp.broadcast_to(k_pe, (B, S, MLA_HEADS, MLA_ROPE))], axis=-1)
    o_mla = causal_block_attention(q, k, kv[..., MLA_NOPE:], (MLA_NOPE + MLA_ROPE) ** -0.5)
    R = SWA_HEADS // SWA_KV_HEADS
    cos_s, sin_s = rope_tables(S, SWA_HEAD_DIM)
    qs = apply_rope(q_s.reshape(B, S, SWA_HEADS, SWA_HEAD_DIM), cos_s, sin_s).reshape(B, S, SWA_KV_HEADS, R, SWA_HEAD_DIM)
    ks = apply_rope(k_s.reshape(B, S, SWA_KV_HEADS, SWA_HEAD_DIM), cos_s, sin_s)
    vs = v_s.reshape(B, S, SWA_KV_HEADS, SWA_HEAD_DIM)
    o_swa = banded_attention(qs, ks, vs, SWA_WINDOW, SWA_HEAD_DIM ** -0.5, sinks)
    o = jnp.concatenate([o_mla.reshape(B, S, -1), o_swa.reshape(B, S, -1)], axis=-1)
    return o @ w_out


def odd_mixer(h, w_in, gate_b, cmp_pos_k, cmp_pos_v, cmp_k_w1, cmp_k_w2, cmp_v_w1, cmp_v_w2, w_out):
    B, S, _ = h.shape
    G, d = NSA_KV_GROUPS, NSA_HEAD_DIM
    R = NSA_HEADS // G
    kvw = G * d
    z = h @ w_in
    q, kc, vc, ks, vs, kw, vw, gl = jnp.split(z, _offsets([NSA_HEADS * d, kvw, kvw, kvw, kvw, kvw, kvw]), axis=-1)
    cos, sin = rope_tables(S, d)
    q = apply_rope(q.reshape(B, S, NSA_HEADS, d), cos, sin).reshape(B, S, G, R, d)
    kc = apply_rope(kc.reshape(B, S, G, d), cos, sin)
    ks = apply_rope(ks.reshape(B, S, G, d), cos, sin)
    kw = apply_rope(kw.reshape(B, S, G, d), cos, sin)
    vc = vc.reshape(B, S, G, d)
    vs = vs.reshape(B, S, G, d)
    vw = vw.reshape(B, S, G, d)
    scale = d ** -0.5
    t = jnp.arange(S)
    kcmp = compress_tokens(kc, cmp_pos_k, cmp_k_w1, cmp_k_w2)
    vcmp = compress_tokens(vc, cmp_pos_v, cmp_v_w1, cmp_v_w2)
    nc = kcmp.shape[1]
    cmp_end = jnp.arange(nc) * CMP_STRIDE + CMP_BLOCK - 1
    cmask = cmp_end[None, :] <= t[:, None]
    s = jnp.einsum('bsgrd,bcgd->bgrsc', q, kcmp).astype(F32) * scale
    p_cmp = jax.nn.softmax(jnp.where(cmask, s, NEG), axis=-1) * cmask
    o_cmp = jnp.einsum('bgrsc,bcgd->bsgrd', p_cmp.astype(h.dtype), vcmp)
    ns = S // SLC_BLOCK
    n_sel = min(N_SEL, ns)
    starts = np.arange(nc) * CMP_STRIDE
    jb = np.arange(ns)
    overlap = ((starts[:, None] < (jb[None, :] + 1) * SLC_BLOCK) & (starts[:, None] + CMP_BLOCK > jb[None, :] * SLC_BLOCK)).astype(np.float32)
    imp = jnp.einsum('bgrsc,cj->bgsj', p_cmp, jnp.asarray(overlap))
    blk = jnp.arange(ns)
    cur = t // SLC_BLOCK
    eligible = blk[None, :] <= cur[:, None]
    forced = (blk[None, :] == 0) | (blk[None, :] == cur[:, None]) | (blk[None, :] == cur[:, None] - 1)
    score = jnp.where(eligible, jnp.where(forced, FORCE_SCORE, imp), -1.0)
    top_val, top_idx = lax.top_k(score, n_sel)
    o_slc = selected_block_attention(q, ks, vs, top_idx, top_val >= 0.0, scale)
    o_win = banded_attention(q, kw, vw, NSA_WINDOW, scale)
    gates = jax.nn.sigmoid(gl + gate_b).reshape(B, S, G, R, 3)
    o = gates[..., 0:1] * o_cmp + gates[..., 1:2] * o_slc + gates[..., 2:3] * o_win
    return o.reshape(B, S, -1) @ w_out


def setup_inputs(seed: int = 0) -> dict:
    key = jax.random.key(seed)
    ks = jax.random.split(key, 24)
    nrm = lambda k, shape, scale: jax.random.normal(k, shape, F32) * scale
    return {
        "x": nrm(ks[0], (BATCH, SEQ, D_MODEL), 1.0),
        "ln_g": 1.0 + nrm(ks[1], (DEPTH, 3, D_MODEL), 0.02),
        "ln_b": nrm(ks[2], (DEPTH, 3, D_MODEL), 0.02),
        "ffn_w_gate": nrm(ks[3], (DEPTH, 2, D_MODEL, D_FF), D_MODEL ** -0.5),
        "ffn_w_up": nrm(ks[4], (DEPTH, 2, D_MODEL, D_FF), D_MODEL ** -0.5),
        "ffn_w_down": nrm(ks[5], (DEPTH, 2, D_FF, D_MODEL), DN_BETA * D_FF ** -0.5),
        "even_w_in": nrm(ks[6], (N_EVEN, D_MODEL, EVEN_IN), D_MODEL ** -0.5),
        "mla_q_norm": 1.0 + nrm(ks[7], (N_EVEN, MLA_Q_RANK), 0.02),
        "mla_w_uq": nrm(ks[8], (N_EVEN, MLA_Q_RANK, MLA_HEADS * (MLA_NOPE + MLA_ROPE)), MLA_Q_RANK ** -0.5),
        "mla_kv_norm": 1.0 + nrm(ks[9], (N_EVEN, MLA_KV_RANK), 0.02),
        "mla_w_ukv": nrm(ks[10], (N_EVEN, MLA_KV_RANK, MLA_HEADS * (MLA_NOPE + MLA_V)), MLA_KV_RANK ** -0.5),
        "swa_sinks": nrm(ks[11], (N_EVEN, SWA_HEADS), 0.5),
        "even_w_out": nrm(ks[12], (N_EVEN, EVEN_OUT, D_MODEL), DN_BETA * EVEN_OUT ** -0.5),
        "odd_w_in": nrm(ks[13], (N_ODD, D_MODEL, ODD_IN), D_MODEL ** -0.5),
        "nsa_gate_b": nrm(ks[14], (N_ODD, 3 * NSA_HEADS), 0.1),
        "nsa_cmp_pos_k": nrm(ks[15], (N_ODD, CMP_BLOCK, NSA_HEAD_DIM), 0.1),
        "nsa_cmp_pos_v": nrm(ks[16], (N_ODD, CMP_BLOCK, NSA_HEAD_DIM), 0.1),
        "nsa_cmp_k_w1": nrm(ks[17], (N_ODD, CMP_BLOCK * NSA_HEAD_DIM, CMP_HIDDEN), (CMP_BLOCK * NSA_HEAD_DIM) ** -0.5),
        "nsa_cmp_k_w2": nrm(ks[18], (N_ODD, CMP_HIDDEN, NSA_HEAD_DIM), CMP_HIDDEN ** -0.5),
        "nsa_cmp_v_w1": nrm(ks[19], (N_ODD, CMP_BLOCK * NSA_HEAD_DIM, CMP_HIDDEN), (CMP_BLOCK * NSA_HEAD_DIM) ** -0.5),
        "nsa_cmp_v_w2": nrm(ks[20], (N_ODD, CMP_HIDDEN, NSA_HEAD_DIM), CMP_HIDDEN ** -0.5),
        "odd_w_out": nrm(ks[21], (N_ODD, ODD_OUT, D_MODEL), DN_BETA * ODD_OUT ** -0.5),
    }


def reference(x, ln_g, ln_b, ffn_w_gate, ffn_w_up, ffn_w_down, even_w_in, mla_q_norm, mla_w_uq, mla_kv_norm, mla_w_ukv, swa_sinks, even_w_out, odd_w_in, nsa_gate_b, nsa_cmp_pos_k, nsa_cmp_pos_v, nsa_cmp_k_w1, nsa_cmp_k_w2, nsa_cmp_v_w1, nsa_cmp_v_w2, odd_w_out):
    for l in range(DEPTH):
        x = layer_norm(DN_ALPHA * x + 0.5 * swiglu(x, ffn_w_gate[l, 0], ffn_w_up[l, 0], ffn_w_down[l, 0]), ln_g[l, 0], ln_b[l, 0])
        j = l // 2
        if l % 2 == 0:
            mix = even_mixer(x, even_w_in[j], mla_q_norm[j], mla_w_uq[j], mla_kv_norm[j], mla_w_ukv[j], swa_sinks[j], even_w_out[j])
        else:
            mix = odd_mixer(x, odd_w_in[j], nsa_gate_b[j], nsa_cmp_pos_k[j], nsa_cmp_pos_v[j], nsa_cmp_k_w1[j], nsa_cmp_k_w2[j], nsa_cmp_v_w1[j], nsa_cmp_v_w2[j], odd_w_out[j])
        x = layer_norm(DN_ALPHA * x + mix, ln_g[l, 1], ln_b[l, 1])
        x = layer_norm(DN_ALPHA * x + 0.5 * swiglu(x, ffn_w_gate[l, 1], ffn_w_up[l, 1], ffn_w_down[l, 1]), ln_g[l, 2], ln_b[l, 2])
    return x
```

```python
import os
import numpy as np
import ml_dtypes
import concourse.bass as bass
import concourse.mybir as mybir
from concourse.bass_utils import run_bass_kernel_spmd

F32 = mybir.dt.float32
BF16 = mybir.dt.bfloat16
ALU = mybir.AluOpType
AF = mybir.ActivationFunctionType
AX = mybir.AxisListType

D = 2048
DFF = 5632
KC = D // 128
FC = DFF // 128
T = 1024
NTB = T // 512
NS = 2048
ALPHA = 4 ** 0.25
LN_EPS = 1e-5
RMS_EPS = 1e-6
PIECE = 4
NPIECE = FC // PIECE
NGU = 2
NWD = 8
NEGB = -30000.0
SAME_SYNC = True
NDS = 16

EVEN_IN = 2368
ODD_IN = 3632


class MK:
    def __init__(self, nc):
        self.nc = nc
        self.E = {'pe': nc.tensor, 'act': nc.scalar, 'dve': nc.vector, 'pool': nc.gpsimd, 'sp': nc.sync}
        self.csem = {e: nc.alloc_semaphore("c_" + e) for e in ('pe', 'act', 'dve', 'pool')}
        self.ccount = {e: 0 for e in self.csem}
        self.waited = {e: {} for e in self.E}
        self.lastw = {}
        self.readers = {}
        self.dsems = {q: [nc.alloc_semaphore("d%s%d" % (q, i)) for i in range(NDS)] for q in ('sp', 'pool')}
        self.dcount = {q: [0] * NDS for q in ('sp', 'pool')}
        self.dtoken = {q: [None] * NDS for q in ('sp', 'pool')}
        self.dnext = {q: 0 for q in ('sp', 'pool')}
        self.banks = [nc.alloc_psum_tensor("psb%d" % i, [128, 512], F32) for i in range(8)]
        self.bctr = {}
        self.n_ops = 0

    def _wait(self, e, tok):
        sem, val, src = tok
        if self.waited[e].get(sem.num, 0) >= val:
            return
        self.E[e].wait_ge(sem, val)
        self.waited[e][sem.num] = val

    def _deps(self, e, reads, writes):
        toks = []
        for k in reads:
            t = self.lastw.get(k)
            if t is not None:
                toks.append(t)
        for k in writes:
            t = self.lastw.get(k)
            if t is not None:
                toks.append(t)
            rd = self.readers.get(k)
            if rd:
                toks.extend(rd.values())
        for t in toks:
            if t[2] == e and (e == 'pe' or not SAME_SYNC):
                continue
            self._wait(e, t)

    def _record(self, tok, reads, writes):
        for k in writes:
            self.lastw[k] = tok
            self.readers[k] = {}
        for k in reads:
            rd = self.readers.setdefault(k, {})
            key = tok[0].num
            if key not in rd or rd[key][1] < tok[1]:
                rd[key] = tok

    def op(self, e, fn, reads=(), writes=()):
        bk = [k for k in reads if isinstance(k, tuple) and k and k[0] == 'bank']
        if bk:
            reads = [k for k in reads if k not in bk]
            writes = list(writes) + bk
        self._deps(e, reads, writes)
        ins = fn(self.E[e])
        self.ccount[e] += 1
        ins.then_inc(self.csem[e], 1)
        tok = (self.csem[e], self.ccount[e], e)
        self._record(tok, reads, writes)
        self.n_ops += 1
        return tok

    def dma(self, q, out, in_, reads=(), writes=()):
        self._deps(q, reads, writes)
        i = self.dnext[q]
        self.dnext[q] = (i + 1) % NDS
        if self.dtoken[q][i] is not None:
            self._wait(q, self.dtoken[q][i])
        ins = self.E[q].dma_start(out=out, in_=in_)
        self.dcount[q][i] += 16
        ins.then_inc(self.dsems[q][i], 16)
        tok = (self.dsems[q][i], self.dcount[q][i], None)
        self.dtoken[q][i] = tok
        self._record(tok, reads, writes)
        return tok

    def bank(self, pool=(0, 1, 2, 3, 4, 5, 6, 7)):
        n = self.bctr.get(pool, 0)
        self.bctr[pool] = n + 1
        i = pool[n % len(pool)]
        return self.banks[i], ('bank', i)

    def barrier(self):
        for e in self.E:
            for t in self.dtoken['sp'] + self.dtoken['pool']:
                if t is not None:
                    self._wait(e, t)
            for e2 in self.csem:
                if self.ccount[e2] > 0 and not (e2 == e and e == 'pe'):
                    self._wait(e, (self.csem[e2], self.ccount[e2], e2))

    def finish(self):
        for t in self.dtoken['sp'] + self.dtoken['pool']:
            if t is not None:
                self._wait('sp', t)
        for e in self.csem:
            if self.ccount[e] > 0:
                self._wait('sp', (self.csem[e], self.ccount[e], e))


class Arena:
    def __init__(self, nc, base, top):
        self.nc, self.base, self.top, self.cur = nc, base, top, base
        self.n = 0

    def alloc(self, name, shape, dtype):
        nbytes = int(np.prod(shape[1:])) * (4 if dtype == F32 else 2)
        nbytes = (nbytes + 31) // 32 * 32
        off = self.cur
        assert off + nbytes <= self.top, ("SBUF arena overflow", name, off, nbytes, self.top)
        self.cur += nbytes
        self.n += 1
        return self.nc.alloc_sbuf_tensor_at("%s_%d" % (name, self.n), list(shape), dtype, offset=off)


def sl512(tb):
    return slice(tb * 512, (tb + 1) * 512)


class Prog:
    def __init__(self, phase):
        self.phase = phase
        nc = self.nc = bass.Bass("TRN2", target_bir_lowering=False)
        self.mk = MK(nc)
        self.din = {}
        self.dout = {}
        base = (nc.sbuf_base + 31) // 32 * 32
        self.A = Arena(nc, base, nc.sbuf_top)
        A = self.A
        self.xb = A.alloc("xb", [128, KC, T], BF16)
        self.rb = [A.alloc("rb", [128, 512], BF16) for _ in range(2)]
        self.rsq = [A.alloc("rsq", [128, 512], BF16) for _ in range(2)]
        self.mean = A.alloc("mean", [128, 512], F32)
        self.msq = A.alloc("msq", [128, 512], F32)
        self.rstd = A.alloc("rstd", [128, 512], F32)
        self.tt = [A.alloc("tt", [128, 512], F32) for _ in range(2)]
        self.ones = A.alloc("ones", [128, 128], BF16)
        self.ident = A.alloc("ident", [128, 128], BF16)
        self.perm128 = A.alloc("perm128", [128, 128], BF16)
        self.perm64 = A.alloc("perm64", [128, 128], BF16)
        self.lng = A.alloc("lng", [128, 6, KC], F32)
        self.lnb = A.alloc("lnb", [128, 6, KC], F32)
        self.lnga = A.alloc("lnga", [128, 6, KC], F32)
        self.lnba = A.alloc("lnba", [128, 6, KC], F32)
        self.ctxb = A.alloc("ctxb", [128, 1], F32)
        self.xr_off = A.cur
        self.xr = A.alloc("xr", [128, KC, T], F32)
        self.ov_off = int(os.environ.get('OV_OFF', A.cur))
        self.ngu = self.nwd = self.nsilu = 0
        self.nws = 0

    def inp(self, name, shape, dtype=F32):
        t = self.nc.dram_tensor(name, list(shape), dtype, kind="ExternalInput").ap()
        self.din[name] = t
        return t

    def outp(self, name, shape, dtype=F32):
        t = self.nc.dram_tensor(name, list(shape), dtype, kind="ExternalOutput").ap()
        self.dout[name] = t
        return t

    def consts(self):
        mk = self.mk
        cst = self.inp("cmat", [3, 128, 128], BF16)
        lg = self.inp("ln_g_t", [128, 6, KC])
        lb = self.inp("ln_b_t", [128, 6, KC])
        cb = self.inp("ctxbias", [128, 1])
        mk.op('dve', lambda e: e.memset(self.ones[:], 1.0), writes=['ones'])
        mk.dma('sp', self.ident[:], cst[0], writes=['ident'])
        mk.dma('sp', self.perm128[:], cst[1], writes=['perm128'])
        mk.dma('sp', self.perm64[:], cst[2], writes=['perm64'])
        mk.dma('sp', self.lng[:], lg, writes=['lng'])
        mk.dma('sp', self.lnb[:], lb, writes=['lnb'])
        mk.dma('sp', self.ctxb[:], cb, writes=['ctxb'])
        mk.op('dve', lambda e: e.tensor_scalar_mul(out=self.lnga[:], in0=self.lng[:], scalar1=ALPHA), reads=['lng'], writes=['lnga'])
        mk.op('dve', lambda e: e.tensor_scalar_mul(out=self.lnba[:], in0=self.lnb[:], scalar1=ALPHA), reads=['lnb'], writes=['lnba'])

    def load_x_raw(self, xT):
        mk = self.mk
        src = xT.rearrange("(c p) t -> p c t", p=128)
        for c in range(KC):
            mk.dma('sp', self.xr[:, c, :], src[:, c, :], writes=[('xr', c, tb) for tb in range(NTB)])
        for c in range(KC):
            for tb in range(NTB):
                s = sl512(tb)
                mk.op('dve', lambda e: e.tensor_copy(out=self.xb[:, c, s], in_=self.xr[:, c, s]),
                      reads=[('xr', c, tb)], writes=[('xb', c, tb)])
                mk.op('act', lambda e: e.mul(out=self.xr[:, c, s], in_=self.xr[:, c, s], mul=ALPHA),
                      reads=[('xr', c, tb)], writes=[('xr', c, tb)])

    def load_xr(self, src_ap):
        src = src_ap.rearrange("(c p) t -> p c t", p=128)
        for c in range(KC):
            self.mk.dma('sp', self.xr[:, c, :], src[:, c, :], writes=[('xr', c, tb) for tb in range(NTB)])

    def load_xb(self, src_ap):
        src = src_ap.rearrange("(c p) t -> p c t", p=128)
        for c in range(KC):
            self.mk.dma('sp', self.xb[:, c, :], src[:, c, :], writes=[('xb', c, tb) for tb in range(NTB)])

    def store_xr(self, dst_ap):
        dst = dst_ap.rearrange("(c p) t -> p c t", p=128)
        for c in range(KC):
            self.mk.dma('sp', dst[:, c, :], self.xr[:, c, :], reads=[('xr', c, tb) for tb in range(NTB)])

    def store_xb(self, dst_ap):
        dst = dst_ap.rearrange("(c p) t -> p c t", p=128)
        for c in range(KC):
            self.mk.dma('sp', dst[:, c, :], self.xb[:, c, :], reads=[('xb', c, tb) for tb in range(NTB)])

    def ffn(self, wg, wu, wd):
        mk = self.mk
        self.A.cur = self.ov_off
        A = self.A
        gT = [A.alloc("gT", [128, PIECE, T], BF16) for _ in range(2)]
        wgs = [A.alloc("wg", [128, KC, 256], BF16) for _ in range(NGU)]
        wus = [A.alloc("wu", [128, KC, 256], BF16) for _ in range(NGU)]
        wds = [A.alloc("wd", [128, D], BF16) for _ in range(NWD)]
        silu = [A.alloc("silu", [128, 512], F32) for _ in range(2)]
        xb, xr = self.xb, self.xr
        wg_v = wg.rearrange("(kc p) f -> p kc f", p=128)
        wu_v = wu.rearrange("(kc p) f -> p kc f", p=128)

        def gate_up(piece):
            gb = piece % 2
            for sp in range(PIECE // 2):
                si = self.ngu % NGU
                self.ngu += 1
                f0 = (piece * PIECE + sp * 2) * 128
                mk.dma('pool', wgs[si][:], wg_v[:, :, f0:f0 + 256], writes=[('wg', si)])
                mk.dma('pool', wus[si][:], wu_v[:, :, f0:f0 + 256], writes=[('wu', si)])
                for fc in range(2):
                    fl = sp * 2 + fc
                    for tb in range(NTB):
                        s = sl512(tb)
                        pg, kg = mk.bank()
                        pu, ku = mk.bank()
                        for kc in range(KC):
                            mk.op('pe', lambda e: e.matmul(pg[:], lhsT=wgs[si][:, kc, fc * 128:(fc + 1) * 128],
                                                           rhs=xb[:, kc, s], start=(kc == 0), stop=(kc == KC - 1)),
                                  reads=[('wg', si), ('xb', kc, tb)], writes=[kg])
                        for kc in range(KC):
                            mk.op('pe', lambda e: e.matmul(pu[:], lhsT=wus[si][:, kc, fc * 128:(fc + 1) * 128],
                                                           rhs=xb[:, kc, s], start=(kc == 0), stop=(kc == KC - 1)),
                                  reads=[('wu', si), ('xb', kc, tb)], writes=[ku])
                        st = self.nsilu % 2
                        self.nsilu += 1
                        mk.op('act', lambda e: e.activation(out=silu[st][:], in_=pg[:], func=AF.Silu),
                              reads=[kg], writes=[('silu', st)])
                        mk.op('dve', lambda e: e.tensor_tensor(out=gT[gb][:, fl, s], in0=silu[st][:], in1=pu[:], op=ALU.mult),
                              reads=[('silu', st), ku], writes=[('gT', gb, fl, tb)])

        def down(piece):
            gb = piece % 2
            slots = []
            for fl in range(PIECE):
                si = self.nwd % NWD
                self.nwd += 1
                r0 = (piece * PIECE + fl) * 128
                mk.dma('pool', wds[si][:], wd[r0:r0 + 128, :], writes=[('wd', si)])
                slots.append(si)
            for tb in range(NTB):
                s = sl512(tb)
                for dq in range(4):
                    bks = [mk.bank() for _ in range(4)]
                    for fl in range(PIECE):
                        si = slots[fl]
                        for dc in range(4):
                            d0 = (dq * 4 + dc) * 128
                            mk.op('pe', lambda e: e.matmul(bks[dc][0][:], lhsT=wds[si][:, d0:d0 + 128],
                                                           rhs=gT[gb][:, fl, s], start=(fl == 0), stop=(fl == PIECE - 1)),
                                  reads=[('wd', si), ('gT', gb, fl, tb)], writes=[bks[dc][1]])
                    for dc in range(4):
                        c = dq * 4 + dc
                        mk.op('dve', lambda e: e.scalar_tensor_tensor(out=xr[:, c, s], in0=bks[dc][0][:], scalar=0.5,
                                                                      in1=xr[:, c, s], op0=ALU.mult, op1=ALU.add),
                              reads=[bks[dc][1], ('xr', c, tb)], writes=[('xr', c, tb)])

        gate_up(0)
        for p in range(NPIECE):
            if p + 1 < NPIECE:
                gate_up(p + 1)
            down(p)

    def ln(self, idx, last=False):
        mk = self.mk
        xr, xb = self.xr, self.xb
        for tb in range(NTB):
            s = sl512(tb)
            ps_s, ks = mk.bank()
            ps_q, kq = mk.bank()
            for c in range(KC):
                j = c % 2
                mk.op('act', lambda e: e.copy(out=self.rb[j][:], in_=xr[:, c, s]),
                      reads=[('xr', c, tb)], writes=[('rb', j)])
                mk.op('act', lambda e: e.activation(out=self.rsq[j][:], in_=xr[:, c, s], func=AF.Square),
                      reads=[('xr', c, tb)], writes=[('rsq', j)])
                mk.op('pe', lambda e: e.matmul(ps_s[:], lhsT=self.ones[:], rhs=self.rb[j][:], start=(c == 0), stop=(c == KC - 1)),
                      reads=['ones', ('rb', j)], writes=[ks])
                mk.op('pe', lambda e: e.matmul(ps_q[:], lhsT=self.ones[:], rhs=self.rsq[j][:], start=(c == 0), stop=(c == KC - 1)),
                      reads=['ones', ('rsq', j)], writes=[kq])
            mk.op('dve', lambda e: e.tensor_scalar_mul(out=self.mean[:], in0=ps_s[:], scalar1=1.0 / D), reads=[ks], writes=['mean'])
            mk.op('dve', lambda e: e.tensor_tensor(out=self.msq[:], in0=self.mean[:], in1=self.mean[:], op=ALU.mult), reads=['mean'], writes=['msq'])
            mk.op('dve', lambda e: e.scalar_tensor_tensor(out=self.rstd[:], in0=ps_q[:], scalar=1.0 / D, in1=self.msq[:],
                                                          op0=ALU.mult, op1=ALU.subtract), reads=[kq, 'msq'], writes=['rstd'])
            mk.op('dve', lambda e: e.tensor_scalar_add(out=self.rstd[:], in0=self.rstd[:], scalar1=LN_EPS), reads=['rstd'], writes=['rstd'])
            mk.op('dve', lambda e: e.reciprocal(out=self.rstd[:], in_=self.rstd[:]), reads=['rstd'], writes=['rstd'])
            mk.op('act', lambda e: e.activation(out=self.rstd[:], in_=self.rstd[:], func=AF.Sqrt), reads=['rstd'], writes=['rstd'])
            for c in range(KC):
                j = c % 2
                mk.op('dve', lambda e: e.tensor_tensor(out=self.tt[j][:], in0=xr[:, c, s], in1=self.mean[:], op=ALU.subtract),
                      reads=[('xr', c, tb), 'mean'], writes=[('tt', j)])
                mk.op('dve', lambda e: e.tensor_tensor(out=self.tt[j][:], in0=self.tt[j][:], in1=self.rstd[:], op=ALU.mult),
                      reads=[('tt', j), 'rstd'], writes=[('tt', j)])
                gA = self.lng if last else self.lnga
                bA = self.lnb if last else self.lnba
                mk.op('act', lambda e: e.activation(out=xr[:, c, s], in_=self.tt[j][:], func=AF.Identity,
                                                    bias=bA[:, idx, c:c + 1], scale=gA[:, idx, c:c + 1]),
                      reads=[('tt', j), 'lnga', 'lnba', 'lng', 'lnb'], writes=[('xr', c, tb)])
                if not last:
                    mk.op('dve', lambda e: e.tensor_scalar(out=xb[:, c, s], in0=self.tt[j][:], scalar1=self.lng[:, idx, c:c + 1],
                                                           scalar2=self.lnb[:, idx, c:c + 1], op0=ALU.mult, op1=ALU.add),
                          reads=[('tt', j), 'lng', 'lnb'], writes=[('xb', c, tb)])

    def alloc_wslots(self, n=4):
        self.ws = [self.A.alloc("ws", [128, KC, 256], BF16) for _ in range(n)]
        self.nws = 0

    def load_w(self, w_ap, kcn, ncols):
        si = self.nws % len(self.ws)
        self.nws += 1
        flat = self.ws[si][:].rearrange("p a b -> p (a b)")
        view = flat[:, 0:kcn * ncols].rearrange("p (a b) -> p a b", b=ncols)
        self.mk.dma('pool', view, w_ap.rearrange("(kc p) f -> p kc f", p=128), writes=[('ws', si)])
        return view, ('ws', si)

    def mm(self, out_ap, okey, pairs, extra_reads=()):
        n = len(pairs)
        for i, (l, r, rk) in enumerate(pairs):
            self.mk.op('pe', lambda e: e.matmul(out_ap, lhsT=l, rhs=r, start=(i == 0), stop=(i == n - 1)),
                       reads=list(rk) + list(extra_reads), writes=[okey])

    def rope_fm(self, ps, pk, npart, perm, cos_ap, sin_ap, out_ap, okeys, tkeys, pool):
        mk = self.mk
        j = self.nrt % 2
        self.nrt += 1
        zb, t1, t2 = self.r_zb[j], self.r_t1[j], self.r_t2[j]
        mk.op('act', lambda e: e.copy(out=zb[:npart, :], in_=ps[:npart, :]), reads=[pk], writes=[('r_zb', j)])
        p2, k2 = mk.bank(pool)
        mk.op('pe', lambda e: e.matmul(p2[:npart, :], lhsT=perm[:npart, :npart], rhs=zb[:npart, :], start=True, stop=True),
              reads=[('r_zb', j), 'perm128', 'perm64'], writes=[k2])
        mk.op('dve', lambda e: e.tensor_tensor(out=t1[:npart, :], in0=ps[:npart, :], in1=cos_ap, op=ALU.mult),
              reads=[pk] + list(tkeys), writes=[('r_t1', j)])
        mk.op('dve', lambda e: e.tensor_tensor(out=t2[:npart, :], in0=p2[:npart, :], in1=sin_ap, op=ALU.mult),
              reads=[k2] + list(tkeys), writes=[('r_t2', j)])
        mk.op('dve', lambda e: e.tensor_tensor(out=out_ap, in0=t1[:npart, :], in1=t2[:npart, :], op=ALU.add),
              reads=[('r_t1', j), ('r_t2', j)], writes=list(okeys))

    def alloc_rope_tmps(self):
        self.r_zb = [self.rb[0], self.rb[1]]
        self.r_t1 = [self.tt[0], self.mean]
        self.r_t2 = [self.tt[1], self.msq]
        self.nrt = 0

    def attn(self, name, q_parts, k_parts, vaug_fn, nv, tiles, scale, finish_fn, pt_bufs, mask_tiles, extra_mask_fn=None):
        mk = self.mk
        PA = (0, 1, 2, 3)
        PB = (4, 5, 6, 7)
        ob = [mk.bank(PB), mk.bank(PB)]
        O = [(ob[qt // 2][0][:, (qt % 2) * 256:(qt % 2) * 256 + nv], ob[qt // 2][1]) for qt in range(4)]
        nt = len(tiles)
        for ti, (kt, bias_ap, mi) in enumerate(tiles):
            sp, sk = mk.bank(PA)
            kp = k_parts(kt)
            pairs = [(kp[i][0], q_parts[i][0], list(kp[i][1]) + list(q_parts[i][1])) for i in range(len(kp))]
            self.mm(sp[:], sk, pairs)
            j = self.npt % len(pt_bufs)
            self.npt += 1
            pt = pt_bufs[j]
            rk = [sk] + (['ctxb', 'cvb'] if bias_ap is not None else [])
            if bias_ap is not None:
                mk.op('act', lambda e: e.activation(out=pt[:], in_=sp[:], func=AF.Exp, bias=bias_ap, scale=scale),
                      reads=rk, writes=[('pt', j)])
            else:
                mk.op('act', lambda e: e.activation(out=pt[:], in_=sp[:], func=AF.Exp, scale=scale),
                      reads=rk, writes=[('pt', j)])
            if mi is not None:
                mk.op('dve', lambda e: e.tensor_tensor(out=pt[:], in0=pt[:], in1=mask_tiles[:, mi, :], op=ALU.mult),
                      reads=[('pt', j), 'masks'], writes=[('pt', j)])
            if extra_mask_fn is not None:
                extra_mask_fn(kt, pt, ('pt', j))
            va, vk = vaug_fn(kt)
            for qt in range(4):
                mk.op('pe', lambda e: e.matmul(O[qt][0], lhsT=pt[:, qt * 128:(qt + 1) * 128], rhs=va,
                                               start=(ti == 0 and qt % 2 == 0), stop=(ti == nt - 1 and qt % 2 == 1)),
                      reads=[('pt', j)] + list(vk), writes=[O[qt][1]])
        for qt in range(4):
            finish_fn(qt, O[qt][0], O[qt][1])

    def rms_block(self, chunk_fn, gvec, out_fn, pool):
        mk = self.mk
        st, stk = mk.bank((4, 5, 6, 7))
        for c in range(4):
            ps, pk = chunk_fn(c)
            j = c % 2
            mk.op('act', lambda e: e.copy(out=self.cf32[:, c, :], in_=ps), reads=[pk], writes=[('cf32', c)])
            mk.op('act', lambda e: e.activation(out=self.rsq[j][:], in_=ps, func=AF.Square), reads=[pk], writes=[('rsq', j)])
            mk.op('pe', lambda e: e.matmul(st[:], lhsT=self.ones[:], rhs=self.rsq[j][:], start=(c == 0), stop=(c == 3)),
                  reads=['ones', ('rsq', j)], writes=[stk])
        mk.op('dve', lambda e: e.tensor_scalar(out=self.rstd[:], in0=st[:], scalar1=1.0 / 512, scalar2=RMS_EPS,
                                               op0=ALU.mult, op1=ALU.add), reads=[stk], writes=['rstd'])
        mk.op('dve', lambda e: e.reciprocal(out=self.rstd[:], in_=self.rstd[:]), reads=['rstd'], writes=['rstd'])
        mk.op('act', lambda e: e.activation(out=self.rstd[:], in_=self.rstd[:], func=AF.Sqrt), reads=['rstd'], writes=['rstd'])
        for c in range(4):
            oa, ok = out_fn(c)
            mk.op('dve', lambda e: e.scalar_tensor_tensor(out=oa, in0=self.cf32[:, c, :], scalar=gvec[:, c:c + 1],
                                                          in1=self.rstd[:], op0=ALU.mult, op1=ALU.mult),
                  reads=[('cf32', c), 'rstd', 'gvecs'], writes=list(ok))

    def transpose_to_xb(self, otm_ap, okey, chunk, tok0, pool):
        mk = self.mk
        bk, kk = mk.bank(pool)
        mk.op('pe', lambda e: e.matmul(bk[:, 0:128], lhsT=otm_ap, rhs=self.ident[:], start=True, stop=True),
              reads=[okey, 'ident'], writes=[kk])
        mk.op('act', lambda e: e.copy(out=self.xb[:, chunk, tok0:tok0 + 128], in_=bk[:, 0:128]),
              reads=[kk], writes=[('xb', chunk, tok0 // 512)])

    def outproj(self, w_out):
        mk = self.mk
        self.A.cur = self.ov_off
        self.alloc_wslots(4)
        for g in range(8):
            wv, wk = self.load_w(w_out[:, g * 256:(g + 1) * 256], KC, 256)
            for dc in range(2):
                c = g * 2 + dc
                for tb in range(NTB):
                    s = sl512(tb)
                    bk, kk = mk.bank()
                    self.mm(bk[:], kk, [(wv[:, kc, dc * 128:(dc + 1) * 128], self.xb[:, kc, s], [wk, ('xb', kc, tb)]) for kc in range(KC)])
                    mk.op('dve', lambda e: e.tensor_tensor(out=self.xr[:, c, s], in0=bk[:], in1=self.xr[:, c, s], op=ALU.add),
                          reads=[kk, ('xr', c, tb)], writes=[('xr', c, tb)])

    def even_attn(self):
        mk = self.mk
        A = self.A
        A.cur = self.xr_off
        PA = (0, 1, 2, 3)
        w_in = self.inp("even_w_in", [D, EVEN_IN])
        w_uq = self.inp("mla_w_uq", [512, 1536])
        w_ukv = self.inp("mla_w_ukv", [512, 2048])
        gv_d = self.inp("gv_e", [128, 8])
        sinks_d = self.inp("sinks_b", [128, 16])
        rope_d = self.inp("rope64", [2, 128, NS], BF16)
        masks_d = self.inp("masks_e", [128, 9, 512], BF16)
        xbfull = self.inp("xbfull", [D, NS], BF16)
        self.alloc_wslots(4)
        self.alloc_rope_tmps()
        hblk = A.alloc("hblk", [128, KC, 512], BF16)
        ckvn = A.alloc("ckvn", [128, 4, NS], BF16)
        kpeT = A.alloc("kpeT", [128, NS], BF16)
        ksT = A.alloc("ksT", [128, NS], BF16)
        vs = A.alloc("vs", [128, 16, 2, 65], BF16)
        ksTb = A.alloc("ksTb", [128, NS], BF16)
        wsw = A.alloc("wsw", [128, KC, 128], BF16)
        cqn = A.alloc("cqn", [128, 4, T], BF16)
        qsT = A.alloc("qsT", [128, 8, T], BF16)
        KnT = A.alloc("KnT", [128, NS], BF16)
        Vh = A.alloc("Vh", [128, 16, 129], BF16)
        qnT = A.alloc("qnT", [128, T], BF16)
        qpT = A.alloc("qpT", [128, T], BF16)
        wkv_h = A.alloc("wkv_h", [128, 4, 256], BF16)
        wq_h = A.alloc("wq_h", [128, 4, 192], BF16)
        pts = [A.alloc("pt", [128, 512], BF16) for _ in range(3)]
        masks = A.alloc("masks", [128, 9, 512], BF16)
        cos64 = A.alloc("cos64", [128, NS], BF16)
        sin64 = A.alloc("sin64", [128, NS], BF16)
        self.cf32 = A.alloc("cf32", [128, 4, 512], F32)
        gv = A.alloc("gv", [128, 8], F32)
        sinke = A.alloc("sinke", [128, 16], F32)
        otm = [A.alloc("otm", [128, 128], BF16) for _ in range(8)]
        den = [A.alloc("den", [128, 1], F32) for _ in range(4)]
        self.npt = 0
        xb = self.xb

        mk.dma('sp', masks[:], masks_d, writes=['masks'])
        mk.dma('sp', cos64[:], rope_d[0], writes=['rope'])
        mk.dma('sp', sin64[:], rope_d[1], writes=['rope'])
        mk.dma('sp', gv[:], gv_d, writes=['gvecs'])
        mk.dma('sp', sinke[:], sinks_d, writes=['sinke'])
        mk.op('act', lambda e: e.activation(out=sinke[:], in_=sinke[:], func=AF.Exp), reads=['sinke'], writes=['sinke'])
        for kt in range(16):
            mk.op('dve', lambda e: e.memset(vs[:, kt, :, 64:65], 1.0), writes=[('vs', kt)])
            mk.op('dve', lambda e: e.memset(Vh[:, kt, 128:129], 1.0), writes=[('Vh1', kt)])
        if os.environ.get('EVEN_STOP') == 'a':
            return

        wa, wak = self.load_w(w_in[:, 512:768], KC, 256)
        wb, wbk = self.load_w(w_in[:, 768:1024], KC, 256)
        wc, wck = self.load_w(w_in[:, 1024:1088], KC, 64)
        wd_, wdk = self.load_w(w_in[:, 2112:2368], KC, 256)
        xbf_v = xbfull.rearrange("(c p) t -> p c t", p=128)
        w_in_v = w_in.rearrange("(kc p) f -> p kc f", p=128)
        mk.dma('pool', wsw[:, :, 0:64], w_in_v[:, :, 2176:2240], writes=['wsw0'])
        mk.dma('pool', wsw[:, :, 64:128], w_in_v[:, :, 2112:2176], writes=['wsw1'])
        if os.environ.get('EVEN_STOP') == 'b':
            return
        for sb in range(4):
            ssl = slice(sb * 512, (sb + 1) * 512)
            for c_ in range(KC):
                mk.dma('pool', hblk[:, c_, :], xbf_v[:, c_, ssl], writes=['hblk'])

            def ckv_chunk(c):
                bk, kk = mk.bank(PA)
                w_, wk_ = (wa, wak) if c < 2 else (wb, wbk)
                self.mm(bk[:], kk, [(w_[:, kc, (c % 2) * 128:(c % 2) * 128 + 128], hblk[:, kc, :], [wk_, 'hblk']) for kc in range(KC)])
                return bk[:], kk
            self.rms_block(ckv_chunk, gv[:, 4:8], lambda c: (ckvn[:, c, ssl], [('ckvn', c, sb)]), PA)
            if os.environ.get('EVEN_STOP') == 'c':
                return
            bk, kk = mk.bank(PA)
            self.mm(bk[:64, :], kk, [(wc[:, kc, 0:64], hblk[:, kc, :], [wck, 'hblk']) for kc in range(KC)])
            self.rope_fm(bk, kk, 64, self.perm64, cos64[:64, ssl], sin64[:64, ssl], kpeT[:64, ssl], [('kpeT', sb)], ['rope'], PA)
            if os.environ.get('EVEN_STOP') == 'd':
                return
            bk, kk = mk.bank(PA)
            _var = os.environ.get('EVEN_VAR', '')
            if _var == 'wa':
                self.mm(bk[:], kk, [(wa[:, kc, 0:128], hblk[:, kc, :], [wak, 'hblk']) for kc in range(KC)])
            else:
                self.mm(bk[:], kk, [(wd_[:, kc, 0:128], hblk[:, kc, :], [wdk, 'hblk']) for kc in range(KC)])
            if _var == 'copy':
                mk.op('act', lambda e: e.copy(out=ksT[:, ssl], in_=bk[:]), reads=[kk], writes=[('ksT', sb)])
            elif _var == 'none':
                pass
            elif _var == 'copy2':
                mk.op('act', lambda e: e.copy(out=self.cf32[:, 0, :], in_=bk[:]), reads=[kk, ('cf32', 0)], writes=[('cf32', 0)])
            elif _var == 'copy3':
                mk.op('act', lambda e: e.copy(out=ksT[:, ssl], in_=self.cf32[:, 0, :]), reads=[kk, ('cf32', 0)], writes=[('ksT', sb)])
            elif _var == 'copy4':
                mk.op('dve', lambda e: e.tensor_copy(out=ksT[:, ssl], in_=bk[:]), reads=[kk], writes=[('ksT', sb)])
            else:
                self.rope_fm(bk, kk, 128, self.perm64, cos64[:, ssl], sin64[:, ssl], ksT[:, ssl], [('ksT', sb)], ['rope'], PA)
            if os.environ.get('EVEN_STOP') == 'e':
                return
            bk, kk = mk.bank(PA)
            self.mm(bk[:], kk, [(wsw[:, kc, :], hblk[:, kc, :], ['wsw0', 'wsw1', 'hblk']) for kc in range(KC)])
            self.rope_fm(bk, kk, 128, self.perm64, cos64[:, ssl], sin64[:, ssl], ksTb[:, ssl], [('ksTb', sb)], ['rope'], PA)
            if os.environ.get('EVEN_STOP') == 'f':
                return
            for tt_ in range(4):
                kt = sb * 4 + tt_
                bk, kk = mk.bank(PA)
                self.mm(bk[:, 0:128], kk, [(hblk[:, kc, tt_ * 128:(tt_ + 1) * 128], wd_[:, kc, 128:256], [wdk, 'hblk']) for kc in range(KC)])
                mk.op('act', lambda e: e.copy(out=vs[:, kt, :, 0:64], in_=bk[:, 0:128].rearrange("p (g d) -> p g d", g=2)),
                      reads=[kk], writes=[('vs', kt)])

        if os.environ.get('EVEN_STOP') == '1':
            return
        w0, w0k = self.load_w(w_in[:, 0:256], KC, 256)
        w1, w1k = self.load_w(w_in[:, 256:512], KC, 256)
        for tb in range(NTB):
            s = sl512(tb)

            def cq_chunk(c):
                bk, kk = mk.bank(PA)
                w_, wk_ = (w0, w0k) if c < 2 else (w1, w1k)
                self.mm(bk[:], kk, [(w_[:, kc, (c % 2) * 128:(c % 2) * 128 + 128], xb[:, kc, s], [wk_, ('xb', kc, tb)]) for kc in range(KC)])
                return bk[:], kk
            self.rms_block(cq_chunk, gv[:, 0:4], lambda c: (cqn[:, c, s], [('cqn', c, tb)]), PA)
        for g in range(4):
            wq_, wqk = self.load_w(w_in[:, 1088 + g * 256:1088 + (g + 1) * 256], KC, 256)
            for cc in range(2):
                c = g * 2 + cc
                for tb in range(NTB):
                    s = sl512(tb)
                    bk, kk = mk.bank(PA)
                    self.mm(bk[:], kk, [(wq_[:, kc, cc * 128:(cc + 1) * 128], xb[:, kc, s], [wqk, ('xb', kc, tb)]) for kc in range(KC)])
                    so = slice(1024 + tb * 512, 1024 + (tb + 1) * 512)
                    self.rope_fm(bk, kk, 128, self.perm64, cos64[:, so], sin64[:, so], qsT[:, c, s], [('qsT', c, tb)], ['rope'], PA)

        if os.environ.get('EVEN_STOP') == '2':
            return
        for qb in range(NTB):
            s = sl512(qb)
            for h in range(16):
                g = h // 8
                hp = (h % 2) * 64
                tiles = []
                for d in range(-1, 4):
                    kt = 8 + 4 * qb + d
                    tiles.append((kt, self.ctxb[:, 0:1] if kt < 8 else None, 4 + (d + 1)))

                def fin(qt, O, ok, h=h, qb=qb):
                    dn = den[qt]
                    mk.op('dve', lambda e: e.tensor_tensor(out=dn[:], in0=O[:, 64:65], in1=sinke[:, h:h + 1], op=ALU.add),
                          reads=[ok, 'sinke'], writes=[('den', qt)])
                    mk.op('dve', lambda e: e.reciprocal(out=dn[:], in_=dn[:]), reads=[('den', qt)], writes=[('den', qt)])
                    ot = otm[qt + 4 * ((h // 2) % 2)]
                    okk = ('otm', qt + 4 * ((h // 2) % 2))
                    mk.op('act', lambda e: e.activation(out=ot[:, (h % 2) * 64:(h % 2) * 64 + 64], in_=O[:, 0:64], func=AF.Copy, scale=dn[:, 0:1]),
                          reads=[ok, ('den', qt)], writes=[okk])
                    if h % 2 == 1:
                        self.transpose_to_xb(ot[:], okk, 8 + h // 2, qb * 512 + qt * 128, PA)
                self.attn("swa", [(qsT[hp:hp + 64, h // 2, s], [('qsT', h // 2, qb)])],
                          lambda kt, g=g, hp=hp: [((ksT if g * 64 == hp else ksTb)[hp:hp + 64, kt * 128:(kt + 1) * 128],
                                                    [('ksT', kt // 4), ('ksTb', kt // 4)])],
                          lambda kt: (vs[:, kt, g, :], [('vs', kt)]), 65, tiles, 64 ** -0.5, fin, pts, masks)

        if os.environ.get('EVEN_STOP') == '3':
            return
        wkv_v = w_ukv.rearrange("(kc p) f -> p kc f", p=128)
        wuq_v = w_uq.rearrange("(kc p) f -> p kc f", p=128)
        for h in range(8):
            mk.dma('pool', wkv_h[:], wkv_v[:, :, h * 256:(h + 1) * 256], writes=['wkv_h'])
            mk.dma('pool', wq_h[:], wuq_v[:, :, h * 192:(h + 1) * 192], writes=['wq_h'])
            for sb in range(4):
                ssl = slice(sb * 512, (sb + 1) * 512)
                bk, kk = mk.bank(PA)
                self.mm(bk[:], kk, [(wkv_h[:, c, 0:128], ckvn[:, c, ssl], ['wkv_h', ('ckvn', c, sb)]) for c in range(4)])
                mk.op('act', lambda e: e.copy(out=KnT[:, ssl], in_=bk[:]), reads=[kk], writes=[('KnT', sb)])
                for tt_ in range(4):
                    kt = sb * 4 + tt_
                    bk, kk = mk.bank(PA)
                    self.mm(bk[:, 0:128], kk, [(ckvn[:, c, kt * 128:(kt + 1) * 128], wkv_h[:, c, 128:256], ['wkv_h', ('ckvn', c, sb)]) for c in range(4)])
                    mk.op('dve', lambda e: e.tensor_copy(out=Vh[:, kt, 0:128], in_=bk[:, 0:128]), reads=[kk], writes=[('Vh', kt)])
            for tb in range(NTB):
                s = sl512(tb)
                bk, kk = mk.bank(PA)
                self.mm(bk[:], kk, [(wq_h[:, c, 0:128], cqn[:, c, s], ['wq_h', ('cqn', c, tb)]) for c in range(4)])
                mk.op('act', lambda e: e.copy(out=qnT[:, s], in_=bk[:]), reads=[kk], writes=[('qnT', tb)])
                bk, kk = mk.bank(PA)
                self.mm(bk[:64, :], kk, [(wq_h[:, c, 128:192], cqn[:, c, s], ['wq_h', ('cqn', c, tb)]) for c in range(4)])
                so = slice(1024 + tb * 512, 1024 + (tb + 1) * 512)
                self.rope_fm(bk, kk, 64, self.perm64, cos64[:64, so], sin64[:64, so], qpT[:64, s], [('qpT', tb)], ['rope'], PA)
            for qb in range(NTB):
                s = sl512(qb)
                tiles = []
                for kt in range(0, 8 + 4 * qb + 4):
                    d = kt - (8 + 4 * qb)
                    tiles.append((kt, self.ctxb[:, 0:1] if kt < 8 else None, d if d >= 0 else None))

                def fin(qt, O, ok, h=h, qb=qb):
                    dn = den[qt]
                    mk.op('dve', lambda e: e.reciprocal(out=dn[:], in_=O[:, 128:129]), reads=[ok], writes=[('den', qt)])
                    ot = otm[qt + 4 * (h % 2)]
                    okk = ('otm', qt + 4 * (h % 2))
                    mk.op('act', lambda e: e.activation(out=ot[:], in_=O[:, 0:128], func=AF.Copy, scale=dn[:, 0:1]),
                          reads=[ok, ('den', qt)], writes=[okk])
                    self.transpose_to_xb(ot[:], okk, h, qb * 512 + qt * 128, PA)
                self.attn("mla", [(qnT[:, s], [('qnT', qb)]), (qpT[:64, s], [('qpT', qb)])],
                          lambda kt: [(KnT[:, kt * 128:(kt + 1) * 128], [('KnT', kt // 4)]),
                                      (kpeT[:64, kt * 128:(kt + 1) * 128], [('kpeT', kt // 4)])],
                          lambda kt: (Vh[:, kt, :], [('Vh', kt), ('Vh1', kt)]), 129, tiles, 192 ** -0.5, fin, pts, masks)


    def load_wp(self, pieces, kcn):
        si = self.nws % len(self.ws)
        self.nws += 1
        ntot = sum(p.shape[1] for p in pieces)
        flat = self.ws[si][:].rearrange("p a b -> p (a b)")
        view = flat[:, 0:kcn * ntot].rearrange("p (a b) -> p a b", b=ntot)
        c0 = 0
        for p in pieces:
            n = p.shape[1]
            self.mk.dma('pool', view[:, :, c0:c0 + n], p.rearrange("(kc p) f -> p kc f", p=128), writes=[('ws', si)])
            c0 += n
        return view, ('ws', si)

    def odd_attn(self):
        mk = self.mk
        A = self.A
        A.cur = self.xr_off
        PA = (0, 1, 2, 3)
        w_in = self.inp("odd_w_in", [D, ODD_IN])
        gb_d = self.inp("gate_b_b", [128, 48])
        posT_d = self.inp("cmp_posT", [2, 128, 32], BF16)
        w1k_d = self.inp("nsa_cmp_k_w1", [4096, 256])
        w1v_d = self.inp("nsa_cmp_v_w1", [4096, 256])
        w2k_d = self.inp("nsa_cmp_k_w2", [256, 128])
        w2v_d = self.inp("nsa_cmp_v_w2", [256, 128])
        rope_d = self.inp("rope128", [2, 128, NS], BF16)
        masks_d = self.inp("masks_o", [128, 14, 512], BF16)
        cvb_d = self.inp("cvb", [128, 1])
        ov_d = self.inp("overlap", [128, 32], BF16)
        eexp_d = self.inp("eexp", [32, 16, 128], BF16)
        sA_d = self.inp("scoreA", [128, 8, 32])
        sB_d = self.inp("scoreB", [128, 8, 32])
        xbfull = self.inp("xbfull", [D, NS], BF16)
        self.alloc_wslots(4)
        self.alloc_rope_tmps()
        hblk = A.alloc("hblk", [128, KC, 512], BF16)
        qT = A.alloc("qT", [128, 16, T], BF16)
        kcT = A.alloc("kcT", [128, NS], BF16)
        ksT = A.alloc("ksT", [128, NS], BF16)
        kwT = A.alloc("kwT", [128, NS], BF16)
        vcT = A.alloc("vcT", [128, NS], BF16)
        vsa = A.alloc("vsa", [128, 16, 129], BF16)
        vwa = A.alloc("vwa", [128, 16, 129], BF16)
        oacc = [A.alloc("oacc", [128, 1024], F32) for _ in range(4)]
        masks = A.alloc("masks", [128, 14, 512], BF16)
        cos = A.alloc("cos128", [128, NS], BF16)
        sin = A.alloc("sin128", [128, NS], BF16)
        pts = [A.alloc("pt", [128, 512], BF16) for _ in range(2)]
        gates = A.alloc("gates", [128, 8, 48], F32)
        gateb = A.alloc("gateb", [128, 48], F32)
        posT = A.alloc("posT", [128, 2, 32], BF16)
        w2k = A.alloc("w2k", [128, 2, 128], BF16)
        w2v = A.alloc("w2v", [128, 2, 128], BF16)
        cvb = A.alloc("cvb", [128, 1], F32)
        eexp = A.alloc("eexp", [32, 16, 128], BF16)
        sA = A.alloc("sA", [128, 8, 32], F32)
        sB = A.alloc("sB", [128, 8, 32], F32)
        GT = [A.alloc("GT", [128, 128], BF16) for _ in range(2)]
        gtmp = [A.alloc("gtmp", [128, 128], F32) for _ in range(3)]
        cbias = A.alloc("cbias", [128, 2], F32)
        kcmpT = A.alloc("kcmpT", [128, 128], BF16)
        vcmpa = A.alloc("vcmpa", [128, 161], BF16)
        imp = [A.alloc("imp", [128, 32], F32) for _ in range(4)]
        score = A.alloc("score", [128, 32], F32)
        m8 = A.alloc("m8", [128, 8], F32)
        selb = A.alloc("selb", [128, 32], BF16)
        selT = A.alloc("selT", [32, 512], BF16)
        den = [A.alloc("den", [128, 1], F32) for _ in range(4)]
        fac = [A.alloc("fac", [128, 1], F32) for _ in range(4)]
        otm = [A.alloc("otm", [128, 128], BF16) for _ in range(4)]
        self.npt = 0
        xb = self.xb
        SC = 128 ** -0.5

        mk.dma('sp', masks[:], masks_d, writes=['masks'])
        mk.dma('sp', cos[:], rope_d[0], writes=['rope'])
        mk.dma('sp', sin[:], rope_d[1], writes=['rope'])
        mk.dma('sp', gateb[:], gb_d, writes=['gateb'])
        mk.dma('sp', posT[:, 0, :], posT_d[0], writes=['posT'])
        mk.dma('sp', posT[:, 1, :], posT_d[1], writes=['posT'])
        mk.dma('pool', w2k[:], w2k_d.rearrange("(c p) f -> p c f", p=128), writes=['w2k'])
        mk.dma('pool', w2v[:], w2v_d.rearrange("(c p) f -> p c f", p=128), writes=['w2v'])
        mk.dma('sp', cvb[:], cvb_d, writes=['cvb'])
        mk.dma('sp', eexp[:], eexp_d, writes=['eexp'])
        mk.dma('sp', sA[:], sA_d, writes=['sA'])
        mk.dma('sp', sB[:], sB_d, writes=['sB'])
        mk.op('dve', lambda e: e.memset(vcmpa[:], 0.0), writes=['vcmpa'])
        mk.op('dve', lambda e: e.memset(vcmpa[:, 128:129], 1.0), writes=['vcmpa'])
        mk.dma('sp', vcmpa[:, 129:161], ov_d, writes=['vcmpa'])
        mk.op('dve', lambda e: e.memset(kcmpT[:], 0.0), writes=['kcmpT'])
        for j in range(2):
            mk.op('dve', lambda e: e.memset(GT[j][:], 0.0), writes=[('GT', j)])
        for kt in range(16):
            mk.op('dve', lambda e: e.memset(vsa[:, kt, 128:129], 1.0), writes=[('vsa1', kt)])
            mk.op('dve', lambda e: e.memset(vwa[:, kt, 128:129], 1.0), writes=[('vwa1', kt)])
        xbf_v = xbfull.rearrange("(c p) t -> p c t", p=128)

        for hp_ in range(8):
            wq_, wqk = self.load_w(w_in[:, hp_ * 256:(hp_ + 1) * 256], KC, 256)
            for cc in range(2):
                h = hp_ * 2 + cc
                for tb in range(NTB):
                    s = sl512(tb)
                    bk, kk = mk.bank(PA)
                    self.mm(bk[:], kk, [(wq_[:, kc, cc * 128:(cc + 1) * 128], xb[:, kc, s], [wqk, ('xb', kc, tb)]) for kc in range(KC)])
                    so = slice(1024 + tb * 512, 1024 + (tb + 1) * 512)
                    self.rope_fm(bk, kk, 128, self.perm128, cos[:, so], sin[:, so], qT[:, h, s], [('qT', h, tb)], ['rope'], PA)
        wgl, wglk = self.load_w(w_in[:, 3584:3632], KC, 48)
        for qt in range(8):
            bk, kk = mk.bank(PA)
            self.mm(bk[:, 0:48], kk, [(xb[:, kc, qt * 128:(qt + 1) * 128], wgl[:, kc, :], [wglk, ('xb', kc, qt // 4)]) for kc in range(KC)])
            mk.op('dve', lambda e: e.tensor_tensor(out=gates[:, qt, :], in0=bk[:, 0:48], in1=gateb[:], op=ALU.add),
                  reads=[kk, 'gateb'], writes=[('gates', qt)])
            mk.op('act', lambda e: e.activation(out=gates[:, qt, :], in_=gates[:, qt, :], func=AF.Sigmoid),
                  reads=[('gates', qt)], writes=[('gates', qt)])

        for g in range(2):
            c0 = 2048 + g * 128
            wA, wAk = self.load_wp([w_in[:, c0:c0 + 128], w_in[:, c0 + 256:c0 + 384]], KC)
            wB, wBk = self.load_wp([w_in[:, c0 + 512:c0 + 640], w_in[:, c0 + 768:c0 + 896]], KC)
            wC, wCk = self.load_wp([w_in[:, c0 + 1024:c0 + 1152], w_in[:, c0 + 1280:c0 + 1408]], KC)
            for sb in range(4):
                ssl = slice(sb * 512, (sb + 1) * 512)
                for c_ in range(KC):
                    mk.dma('pool', hblk[:, c_, :], xbf_v[:, c_, ssl], writes=['hblk'])
                for (w_, wk_, dst, dk) in ((wA, wAk, kcT, 'kcT'), (wB, wBk, ksT, 'ksT'), (wC, wCk, kwT, 'kwT')):
                    bk, kk = mk.bank(PA)
                    self.mm(bk[:], kk, [(w_[:, kc, 0:128], hblk[:, kc, :], [wk_, 'hblk']) for kc in range(KC)])
                    self.rope_fm(bk, kk, 128, self.perm128, cos[:, ssl], sin[:, ssl], dst[:, ssl], [(dk, sb)], ['rope'], PA)
                bk, kk = mk.bank(PA)
                self.mm(bk[:], kk, [(wA[:, kc, 128:256], hblk[:, kc, :], [wAk, 'hblk']) for kc in range(KC)])
                mk.op('act', lambda e: e.copy(out=vcT[:, ssl], in_=bk[:]), reads=[kk], writes=[('vcT', sb)])
                for tt_ in range(4):
                    kt = sb * 4 + tt_
                    for (w_, wk_, dst, dk) in ((wB, wBk, vsa, 'vsa'), (wC, wCk, vwa, 'vwa')):
                        bk, kk = mk.bank(PA)
                        self.mm(bk[:, 0:128], kk, [(hblk[:, kc, tt_ * 128:(tt_ + 1) * 128], w_[:, kc, 128:256], [wk_, 'hblk']) for kc in range(KC)])
                        mk.op('act', lambda e: e.copy(out=dst[:, kt, 0:128], in_=bk[:, 0:128]), reads=[kk], writes=[(dk, kt)])

            for which in range(2):
                w1d = w1k_d if which == 0 else w1v_d
                srcT = kcT if which == 0 else vcT
                skey = 'kcT' if which == 0 else 'vcT'
                w2 = w2k if which == 0 else w2v
                w1a, w1ak = self.load_w(w1d[0:2048, :], 16, 256)
                w1b, w1bk = self.load_w(w1d[2048:4096, :], 16, 256)

                def w1l(l):
                    return (w1a[:, l, :], w1ak) if l < 16 else (w1b[:, l - 16, :], w1bk)
                for hc in range(2):
                    bb, bbk = mk.bank(PA)
                    self.mm(bb[:, 0:1], bbk, [(w1l(l)[0][:, hc * 128:(hc + 1) * 128], posT[:, which, l:l + 1], [w1l(l)[1], 'posT']) for l in range(32)])
                    mk.op('act', lambda e: e.copy(out=cbias[:, hc:hc + 1], in_=bb[:, 0:1]), reads=[bbk], writes=[('cbias', hc)])
                    bk, kk = mk.bank(PA)
                    self.mm(bk[:, 0:127], kk, [(w1l(l)[0][:, hc * 128:(hc + 1) * 128], srcT[:, l:l + 16 * 126 + 1:16],
                                                [w1l(l)[1]] + [(skey, sb) for sb in range(4)]) for l in range(32)])
                    xs, u, sg = gtmp
                    mk.op('act', lambda e: e.activation(out=xs[:, 0:127], in_=bk[:, 0:127], func=AF.Identity, bias=cbias[:, hc:hc + 1]),
                          reads=[kk, ('cbias', hc)], writes=['g_xs'])
                    mk.op('dve', lambda e: e.tensor_tensor(out=u[:, 0:127], in0=xs[:, 0:127], in1=xs[:, 0:127], op=ALU.mult),
                          reads=['g_xs'], writes=['g_u'])
                    mk.op('dve', lambda e: e.tensor_scalar(out=u[:, 0:127], in0=u[:, 0:127], scalar1=0.044715, scalar2=1.0,
                                                           op0=ALU.mult, op1=ALU.add), reads=['g_u'], writes=['g_u'])
                    mk.op('dve', lambda e: e.tensor_tensor(out=u[:, 0:127], in0=u[:, 0:127], in1=xs[:, 0:127], op=ALU.mult),
                          reads=['g_u', 'g_xs'], writes=['g_u'])
                    mk.op('act', lambda e: e.activation(out=sg[:, 0:127], in_=u[:, 0:127], func=AF.Sigmoid, scale=1.5957691216057308),
                          reads=['g_u'], writes=['g_sg'])
                    mk.op('dve', lambda e: e.tensor_tensor(out=GT[hc][:, 0:127], in0=xs[:, 0:127], in1=sg[:, 0:127], op=ALU.mult),
                          reads=['g_xs', 'g_sg'], writes=[('GT', hc)])
                if which == 0:
                    bk, kk = mk.bank(PA)
                    self.mm(bk[:, 0:127], kk, [(w2[:, hc, :], GT[hc][:, 0:127], ['w2k', ('GT', hc)]) for hc in range(2)])
                    mk.op('act', lambda e: e.copy(out=kcmpT[:, 0:127], in_=bk[:, 0:127]), reads=[kk], writes=['kcmpT'])
                else:
                    bk, kk = mk.bank(PA)
                    self.mm(bk[0:127, 0:128], kk, [(GT[hc][:, 0:127], w2[:, hc, :], ['w2v', ('GT', hc)]) for hc in range(2)])
                    mk.op('act', lambda e: e.copy(out=vcmpa[0:127, 0:128], in_=bk[0:127, 0:128]), reads=[kk], writes=['vcmpa'])

            for qb in range(NTB):
                s = sl512(qb)
                for hh in range(8):
                    h = g * 8 + hh

                    def fin_c(qt, O, ok, h=h, hh=hh, qb=qb):
                        gq = qb * 4 + qt
                        dn, fc_ = den[qt], fac[qt]
                        mk.op('dve', lambda e: e.tensor_scalar_max(out=dn[:], in0=O[:, 128:129], scalar1=1e-30), reads=[ok], writes=[('den', qt)])
                        mk.op('dve', lambda e: e.reciprocal(out=dn[:], in_=dn[:]), reads=[('den', qt)], writes=[('den', qt)])
                        mk.op('dve', lambda e: e.tensor_tensor(out=fc_[:], in0=dn[:], in1=gates[:, gq, h * 3:h * 3 + 1], op=ALU.mult),
                              reads=[('den', qt), ('gates', gq)], writes=[('fac', qt)])
                        mk.op('act', lambda e: e.activation(out=oacc[qt][:, hh * 128:(hh + 1) * 128], in_=O[:, 0:128], func=AF.Copy, scale=fc_[:, 0:1]),
                              reads=[ok, ('fac', qt)], writes=[('oacc', qt, hh)])
                        if hh == 0:
                            mk.op('dve', lambda e: e.tensor_scalar(out=imp[qt][:], in0=O[:, 129:161], scalar1=dn[:, 0:1], scalar2=None, op0=ALU.mult),
                                  reads=[ok, ('den', qt)], writes=[('imp', qt)])
                        else:
                            mk.op('dve', lambda e: e.scalar_tensor_tensor(out=imp[qt][:], in0=O[:, 129:161], scalar=dn[:, 0:1], in1=imp[qt][:],
                                                                          op0=ALU.mult, op1=ALU.add),
                                  reads=[ok, ('den', qt), ('imp', qt)], writes=[('imp', qt)])
                    self.attn("cmp", [(qT[:, h, s], [('qT', h, qb)])],
                              lambda kt: [(kcmpT[:], ['kcmpT'])],
                              lambda kt: (vcmpa[:], ['vcmpa']), 161, [(0, cvb[:, 0:1], 12 + qb)], SC, fin_c, pts, masks)
                for qt in range(4):
                    gq = qb * 4 + qt
                    mk.op('dve', lambda e: e.tensor_tensor(out=score[:], in0=imp[qt][:], in1=sA[:, gq, :], op=ALU.mult),
                          reads=[('imp', qt), 'sA'], writes=['score'])
                    mk.op('dve', lambda e: e.tensor_tensor(out=score[:], in0=score[:], in1=sB[:, gq, :], op=ALU.add),
                          reads=['score', 'sB'], writes=['score'])
                    mk.op('dve', lambda e: e.max(out=m8[:], in_=score[:]), reads=['score'], writes=['m8'])
                    mk.op('dve', lambda e: e.tensor_scalar_max(out=m8[:, 7:8], in0=m8[:, 7:8], scalar1=0.0), reads=['m8'], writes=['m8'])
                    mk.op('dve', lambda e: e.tensor_scalar(out=selb[:], in0=score[:], scalar1=m8[:, 7:8], scalar2=None, op0=ALU.is_ge),
                          reads=['score', 'm8'], writes=['selb'])
                    bk, kk = mk.bank(PA)
                    mk.op('pe', lambda e: e.matmul(bk[0:32, 0:128], lhsT=selb[:], rhs=self.ident[:], start=True, stop=True),
                          reads=['selb', 'ident'], writes=[kk])
                    mk.op('act', lambda e: e.copy(out=selT[:, qt * 128:(qt + 1) * 128], in_=bk[0:32, 0:128]), reads=[kk], writes=['selT'])
                for hh in range(8):
                    h = g * 8 + hh

                    def mk_fin(br, h=h, hh=hh, qb=qb):
                        def fin(qt, O, ok):
                            gq = qb * 4 + qt
                            dn, fc_ = den[qt], fac[qt]
                            mk.op('dve', lambda e: e.tensor_scalar_max(out=dn[:], in0=O[:, 128:129], scalar1=1e-30), reads=[ok], writes=[('den', qt)])
                            mk.op('dve', lambda e: e.reciprocal(out=dn[:], in_=dn[:]), reads=[('den', qt)], writes=[('den', qt)])
                            mk.op('dve', lambda e: e.tensor_tensor(out=fc_[:], in0=dn[:], in1=gates[:, gq, h * 3 + br:h * 3 + br + 1], op=ALU.mult),
                                  reads=[('den', qt), ('gates', gq)], writes=[('fac', qt)])
                            mk.op('dve', lambda e: e.scalar_tensor_tensor(out=oacc[qt][:, hh * 128:(hh + 1) * 128], in0=O[:, 0:128], scalar=fc_[:, 0:1],
                                                                          in1=oacc[qt][:, hh * 128:(hh + 1) * 128], op0=ALU.mult, op1=ALU.add),
                                  reads=[ok, ('fac', qt), ('oacc', qt, hh)], writes=[('oacc', qt, hh)])
                        return fin

                    def selmask(kt, pt, ptk):
                        bk, kk = mk.bank(PA)
                        mk.op('pe', lambda e: e.matmul(bk[:], lhsT=eexp[:, kt, :], rhs=selT[:], start=True, stop=True),
                              reads=['eexp', 'selT'], writes=[kk])
                        mk.op('dve', lambda e: e.tensor_tensor(out=pt[:], in0=pt[:], in1=bk[:], op=ALU.mult),
                              reads=[ptk, kk], writes=[ptk])
                    tiles = []
                    for kt in range(0, 8 + 4 * qb + 4):
                        d = kt - (8 + 4 * qb)
                        tiles.append((kt, self.ctxb[:, 0:1] if kt < 8 else None, d if d >= 0 else None))
                    self.attn("slc", [(qT[:, h, s], [('qT', h, qb)])],
                              lambda kt: [(ksT[:, kt * 128:(kt + 1) * 128], [('ksT', kt // 4)])],
                              lambda kt: (vsa[:, kt, :], [('vsa', kt), ('vsa1', kt)]), 129, tiles, SC, mk_fin(1), pts, masks, selmask)
                    tiles = []
                    for d in range(-4, 4):
                        kt = 8 + 4 * qb + d
                        tiles.append((kt, self.ctxb[:, 0:1] if kt < 8 else None, 4 + (d + 4)))
                    self.attn("win", [(qT[:, h, s], [('qT', h, qb)])],
                              lambda kt: [(kwT[:, kt * 128:(kt + 1) * 128], [('kwT', kt // 4)])],
                              lambda kt: (vwa[:, kt, :], [('vwa', kt), ('vwa1', kt)]), 129, tiles, SC, mk_fin(2), pts, masks)
                for qt in range(4):
                    for hh in range(8):
                        j = (qt * 8 + hh) % 4
                        mk.op('act', lambda e: e.copy(out=otm[j][:], in_=oacc[qt][:, hh * 128:(hh + 1) * 128]),
                              reads=[('oacc', qt, hh)], writes=[('otm', j)])
                        self.transpose_to_xb(otm[j][:], ('otm', j), g * 8 + hh, qb * 512 + qt * 128, PA)


def bf(a):
    return np.ascontiguousarray(a).astype(ml_dtypes.bfloat16)


def h_cmat():
    ident = np.eye(128, dtype=np.float32)
    k = np.arange(128)[:, None]
    m = np.arange(128)[None, :]
    perm128 = (k == (m + 64) % 128).astype(np.float32)
    perm64 = ((k // 64 == m // 64) & (k % 64 == (m % 64 + 32) % 64)).astype(np.float32)
    return bf(np.stack([ident, perm128, perm64]))


def h_vec16(v):
    v = np.asarray(v, np.float32).reshape(-1, KC, 128)
    return np.ascontiguousarray(v.transpose(2, 0, 1))


def h_rope(dim, t0):
    pos = np.maximum(np.arange(NS) + t0 - 1024, 0).astype(np.float32)
    inv = (1.0 / (10000.0 ** (np.arange(0, dim, 2, dtype=np.float32) / dim))).astype(np.float32)
    ang = pos[None, :] * inv[:, None]
    p = np.arange(128) % dim
    i = p % (dim // 2)
    sign = np.where(p < dim // 2, -1.0, 1.0)[:, None]
    return bf(np.stack([np.cos(ang)[i], sign * np.sin(ang)[i]]))


def h_masks(specs):
    p = np.arange(128)[:, None]
    c = np.arange(512)[None, :]
    out = []
    for d, W in specs:
        rel = c - p - 128 * d
        out.append(((rel >= 0) & (rel < W)).astype(np.float32))
    return bf(np.stack(out, axis=1))


MASKS_E = [(d, 10 ** 9) for d in range(4)] + [(d, 128) for d in range(-1, 4)]


MASKS_O = [(d, 10 ** 9) for d in range(4)] + [(d, 512) for d in range(-4, 4)]


def h_masks_o():
    m = h_masks(MASKS_O).astype(np.float32)
    c = np.arange(128)[:, None]
    col = np.arange(512)[None, :]
    cm = [((16 * c + 31) <= (1024 + 512 * qb + col)).astype(np.float32) for qb in range(2)]
    return bf(np.concatenate([m, np.stack(cm, axis=1)], axis=1))


def h_nsa_consts(t0):
    c = np.arange(128)
    cvb = np.where(((t0 == 1024) | (c >= 64)) & (c < 127), 0.0, NEGB).astype(np.float32)[:, None]
    j = np.arange(32)[None, :]
    ov = ((16 * c[:, None] < 64 * (j + 1)) & (16 * c[:, None] + 32 > 64 * j) & (c[:, None] < 127)).astype(np.float32)
    kt = np.arange(16)[None, :, None]
    m = np.arange(128)[None, None, :]
    eexp = (np.arange(32)[:, None, None] == 2 * kt + m // 64).astype(np.float32)
    i = np.arange(1024)[:, None]
    t = t0 + i
    jb = j - (0 if t0 == 1024 else 16)
    cur = t // 64
    elig = (jb >= 0) & (jb <= cur)
    forced = elig & ((jb == 0) | (jb == cur) | (jb == cur - 1))
    A_ = (elig & ~forced).astype(np.float32)
    B_ = (1e9 * forced - 1.0 * (~elig)).astype(np.float32)
    lay = lambda a: np.ascontiguousarray(a.reshape(8, 128, 32).transpose(1, 0, 2))
    return dict(cvb=cvb, overlap=bf(ov), eexp=bf(eexp), scoreA=lay(A_), scoreB=lay(B_))


_PROGS = {}


def build_phase(phase):
    if phase in _PROGS:
        return _PROGS[phase]
    P = Prog(phase)
    P.consts()
    nff = 2 if phase == 'B' else 1
    wgs = [P.inp("ffn_wg%d" % i, [D, DFF]) for i in range(nff)]
    wus = [P.inp("ffn_wu%d" % i, [D, DFF]) for i in range(nff)]
    wds = [P.inp("ffn_wd%d" % i, [DFF, D]) for i in range(nff)]
    if phase == 'A':
        xT = P.inp("xT", [D, T])
        P.load_x_raw(xT)
        P.ffn(wgs[0], wus[0], wds[0])
        P.ln(0)
        P.store_xr(P.outp("xr_out", [D, T]))
        P.store_xb(P.outp("xb_out", [D, T], BF16))
    elif phase == 'B':
        xb_in = P.inp("xb_in", [D, T], BF16)
        xr_in = P.inp("xr_in", [D, T])
        w_out = P.inp("even_w_out", [D, D])
        P.load_xb(xb_in)
        P.even_attn()
        P.mk.barrier()
        P.load_xr(xr_in)
        P.outproj(w_out)
        P.ln(1)
        P.mk.barrier()
        P.ffn(wgs[0], wus[0], wds[0])
        P.ln(2)
        P.ffn(wgs[1], wus[1], wds[1])
        P.ln(3)
        P.store_xr(P.outp("xr_out", [D, T]))
        P.store_xb(P.outp("xb_out", [D, T], BF16))
    elif phase == 'C':
        xb_in = P.inp("xb_in", [D, T], BF16)
        xr_in = P.inp("xr_in", [D, T])
        w_out = P.inp("odd_w_out", [D, D])
        P.load_xb(xb_in)
        P.odd_attn()
        P.mk.barrier()
        P.load_xr(xr_in)
        P.outproj(w_out)
        P.ln(4)
        P.mk.barrier()
        P.ffn(wgs[0], wus[0], wds[0])
        P.ln(5, last=True)
        P.store_xr(P.outp("outT", [D, T]))
    P.mk.finish()
    _PROGS[phase] = P
    return P


def _run(P, maps):
    maps = [{k: m[k] for k in P.din} for m in maps]
    res = run_bass_kernel_spmd(P.nc, maps, core_ids=list(range(8)))
    return res.results


def kernel(x, ln_g, ln_b, ffn_w_gate, ffn_w_up, ffn_w_down, even_w_in, mla_q_norm, mla_w_uq, mla_kv_norm, mla_w_ukv,
           swa_sinks, even_w_out, odd_w_in, nsa_gate_b, nsa_cmp_pos_k, nsa_cmp_pos_v, nsa_cmp_k_w1, nsa_cmp_k_w2,
           nsa_cmp_v_w1, nsa_cmp_v_w2, odd_w_out):
    f32 = lambda a: np.ascontiguousarray(np.asarray(a, dtype=np.float32))
    x = f32(x)
    cores = [(b, hf) for b in range(4) for hf in range(2)]
    common = {
        "cmat": h_cmat(),
        "ln_g_t": h_vec16(f32(ln_g).reshape(6, D)), "ln_b_t": h_vec16(f32(ln_b).reshape(6, D)),
        "even_w_in": f32(even_w_in)[0], "mla_w_uq": f32(mla_w_uq)[0], "mla_w_ukv": f32(mla_w_ukv)[0],
        "gv_e": np.ascontiguousarray(np.concatenate([f32(mla_q_norm)[0].reshape(4, 128), f32(mla_kv_norm)[0].reshape(4, 128)], 0).T),
        "sinks_b": np.ascontiguousarray(np.broadcast_to(f32(swa_sinks)[0][None, :], (128, 16))),
        "masks_e": h_masks(MASKS_E), "even_w_out": f32(even_w_out)[0],
        "odd_w_in": f32(odd_w_in)[0], "odd_w_out": f32(odd_w_out)[0],
        "gate_b_b": np.ascontiguousarray(np.broadcast_to(f32(nsa_gate_b)[0][None, :], (128, 48))),
        "cmp_posT": bf(np.stack([f32(nsa_cmp_pos_k)[0].T, f32(nsa_cmp_pos_v)[0].T])),
        "nsa_cmp_k_w1": f32(nsa_cmp_k_w1)[0], "nsa_cmp_v_w1": f32(nsa_cmp_v_w1)[0],
        "nsa_cmp_k_w2": f32(nsa_cmp_k_w2)[0], "nsa_cmp_v_w2": f32(nsa_cmp_v_w2)[0],
        "masks_o": h_masks_o(),
    }
    per = []
    for (b, hf) in cores:
        t0 = hf * 1024
        d = dict(common)
        d["ctxbias"] = np.full((128, 1), 0.0 if hf == 1 else NEGB, np.float32)
        d["rope64"] = h_rope(64, t0)
        d["rope128"] = h_rope(128, t0)
        d.update(h_nsa_consts(t0))
        d["xT"] = np.ascontiguousarray(x[b, t0:t0 + 1024, :].T)
        per.append(d)

    def exchange(res):
        for i, (b, hf) in enumerate(cores):
            own = res[i]["xb_out"]
            ctx = res[2 * b]["xb_out"] if hf == 1 else np.zeros_like(own)
            per[i]["xbfull"] = np.ascontiguousarray(np.concatenate([ctx, own], axis=1))
            per[i]["xb_in"] = own
            per[i]["xr_in"] = res[i]["xr_out"]

    G, U, Dn = f32(ffn_w_gate), f32(ffn_w_up), f32(ffn_w_down)

    def setff(lst):
        for d in per:
            for k in ("ffn_wg0", "ffn_wu0", "ffn_wd0", "ffn_wg1", "ffn_wu1", "ffn_wd1"):
                d.pop(k, None)
            for i, (l, j) in enumerate(lst):
                d["ffn_wg%d" % i], d["ffn_wu%d" % i], d["ffn_wd%d" % i] = G[l, j], U[l, j], Dn[l, j]
    setff([(0, 0)])
    rA = _run(build_phase('A'), per)
    exchange(rA)
    setff([(0, 1), (1, 0)])
    rB = _run(build_phase('B'), per)
    exchange(rB)
    setff([(1, 1)])
    rC = _run(build_phase('C'), per)
    out = np.empty((4, 2048, D), np.float32)
    for i, (b, hf) in enumerate(cores):
        out[b, hf * 1024:(hf + 1) * 1024, :] = rC[i]["outT"].T
    return out
```
